# Optimizing a Trainium2 kernel written in Bass

```python
import math
import jax, jax.numpy as jnp
from jax import lax
import numpy as np

D_MODEL = 4096
BATCH = 1
SEQ = 8192
DEPTH = 1

D_MIX = D_MODEL
POOL_WIDTH = D_MIX // 4
POOL_WINDOWS = (2, 4, 8, 16)
POOL_GROUP = POOL_WIDTH // len(POOL_WINDOWS)
HEAD_DIM = 128
NSA_HEADS = (D_MIX - POOL_WIDTH) // HEAD_DIM
KV_GROUPS = 4
HEADS_PER_GROUP = NSA_HEADS // KV_GROUPS
KV_WIDTH = KV_GROUPS * HEAD_DIM
CMP_BLOCK = 32
CMP_STRIDE = 16
SLC_BLOCK = 64
N_SELECT = 16
N_LOCAL_FORCED = 2
WINDOW = 512
N_BRANCH = 3
Q_BLOCK = 128
REL_BUCKETS = 32
REL_MAX_DIST = 128
D_FF = 11008
CONV_WIDTH = 3
RMS_EPS = 1e-6
NEG = -1e30
N_IN = POOL_WIDTH + NSA_HEADS * HEAD_DIM + 6 * KV_WIDTH + NSA_HEADS * N_BRANCH

kernel_name = "hybrid_pool_nsa_convffn"


def _in_splits():
    sizes = [POOL_WIDTH, NSA_HEADS * HEAD_DIM] + [KV_WIDTH] * 6
    out, acc = [], 0
    for s in sizes:
        acc += s
        out.append(acc)
    return out


def _slc_overlap_weights():
    r = SLC_BLOCK // CMP_STRIDE
    lead = -(-CMP_BLOCK // CMP_STRIDE) - 1
    out = []
    for o in range(-lead, r):
        s0 = o * CMP_STRIDE
        ov = max(0, min(s0 + CMP_BLOCK, SLC_BLOCK) - max(s0, 0))
        if ov > 0:
            out.append((o, ov / CMP_STRIDE))
    return lead, out


def rms_norm(x, g):
    xf = x.astype(jnp.float32)
    y = xf * lax.rsqrt(jnp.mean(xf * xf, axis=-1, keepdims=True) + RMS_EPS)
    return (y * g.astype(jnp.float32)).astype(x.dtype)


def rel_bucket(dist):
    n = jnp.maximum(dist, 0)
    max_exact = REL_BUCKETS // 2
    nf = jnp.maximum(n, 1).astype(jnp.float32)
    large = max_exact + (jnp.log(nf / max_exact) / math.log(REL_MAX_DIST / max_exact)
                         * (REL_BUCKETS - max_exact)).astype(jnp.int32)
    large = jnp.minimum(large, REL_BUCKETS - 1)
    return jnp.where(n < max_exact, n, large)


def masked_softmax(s, valid):
    p = jax.nn.softmax(jnp.where(valid, s, NEG), axis=-1)
    return jnp.where(valid, p, 0.0)


def multiscale_pool(u, pool_w, pool_scale):
    B, S, _ = u.shape
    ug = u.astype(jnp.float32).reshape(B, S, len(POOL_WINDOWS), POOL_GROUP)
    c = jnp.cumsum(ug, axis=1)
    c = jnp.concatenate([jnp.zeros_like(c[:, :1]), c], axis=1)
    t = jnp.arange(S)
    outs = []
    for gi, w in enumerate(POOL_WINDOWS):
        lo = jnp.maximum(t + 1 - w, 0)
        cnt = (t + 1 - lo).astype(jnp.float32)
        mean = (c[:, 1:, gi] - c[:, lo, gi]) / cnt[None, :, None]
        outs.append(mean - ug[:, :, gi])
    d = jnp.stack(outs, axis=2).astype(u.dtype)
    y = jnp.einsum('bsgc,gce->bsge', d, pool_w)
    return y.reshape(B, S, POOL_WIDTH) * pool_scale


def compress(kv, pe, w1, w2):
    B, S, G, Dh = kv.shape
    r = CMP_BLOCK // CMP_STRIDE
    ch = kv.reshape(B, S // CMP_STRIDE, CMP_STRIDE, G, Dh)
    nc = S // CMP_STRIDE - r + 1
    blocks = jnp.concatenate([ch[:, j:j + nc] for j in range(r)], axis=2)
    blocks = blocks + pe[None, None, :, None, :]
    h = jax.nn.silu(jnp.einsum('bnlgd,lde->bnge', blocks, w1))
    return jnp.einsum('bnge,ef->bngf', h, w2)


def native_sparse_attention(q, k_cmp, v_cmp, k_slc, v_slc, k_win, v_win, gate_logits,
                            pe_k, w1_k, w2_k, pe_v, w1_v, w2_v, rel_bias):
    B, S, H, Dh = q.shape
    G, HPG = KV_GROUPS, HEADS_PER_GROUP
    dt = q.dtype
    kc = compress(k_cmp, pe_k, w1_k, w2_k)
    vc = compress(v_cmp, pe_v, w1_v, w2_v)
    NC = kc.shape[1]
    NS = S // SLC_BLOCK
    n_sel = min(N_SELECT, NS)
    r = SLC_BLOCK // CMP_STRIDE
    lead, slc_w = _slc_overlap_weights()
    right = r * NS + lead - (NC + lead)
    ks_t = k_slc.reshape(B, NS, SLC_BLOCK, G, Dh).transpose(0, 3, 1, 2, 4)
    vs_t = v_slc.reshape(B, NS, SLC_BLOCK, G, Dh).transpose(0, 3, 1, 2, 4)
    kw_pad = jnp.pad(k_win, ((0, 0), (WINDOW, 0), (0, 0), (0, 0)))
    vw_pad = jnp.pad(v_win, ((0, 0), (WINDOW, 0), (0, 0), (0, 0)))
    gates = jax.nn.sigmoid(gate_logits.astype(jnp.float32)).astype(dt)
    gates = gates.reshape(B, S, G, HPG, N_BRANCH)
    tbl = rel_bias.astype(jnp.float32).T.reshape(G, HPG, REL_BUCKETS)
    scale = HEAD_DIM ** -0.5
    nq = S // Q_BLOCK
    cmp_end = jnp.arange(NC) * CMP_STRIDE + CMP_BLOCK - 1
    jj = jnp.arange(NS)
    bi = jnp.arange(B)[:, None, None, None]
    gi = jnp.arange(G)[None, :, None, None]
    g6 = jnp.arange(G)[None, :, None, None, None, None]
    h6 = jnp.arange(HPG)[None, None, :, None, None, None]

    def block(args):
        i, qb, gb = args
        t = i * Q_BLOCK + jnp.arange(Q_BLOCK)
        qs = qb * scale
        dist_c = t[:, None] - cmp_end[None, :]
        s_c = jnp.einsum('bqghd,bngd->bghqn', qs, kc).astype(jnp.float32)
        s_c = s_c + tbl[:, :, rel_bucket(dist_c)]
        p_c = masked_softmax(s_c, dist_c >= 0)
        o_cmp = jnp.einsum('bghqn,bngd->bqghd', p_c.astype(dt), vc)
        imp = jnp.pad(p_c.sum(axis=2), ((0, 0), (0, 0), (0, 0), (lead, right)))
        slc = sum(w * lax.slice_in_dim(imp, o + lead, o + lead + r * (NS - 1) + 1, r, axis=-1)
                  for o, w in slc_w)
        cur = (t // SLC_BLOCK)[:, None]
        future = (jj[None, :] * SLC_BLOCK) > t[:, None]
        forced = (jj[None, :] == 0) | ((cur - jj[None, :] >= 0) & (cur - jj[None, :] < N_LOCAL_FORCED))
        score = jnp.where(forced, 1e9, jnp.where(future, -1e9, slc))
        _, idx = lax.top_k(score, n_sel)
        kg = ks_t[bi, gi, idx]
        vg = vs_t[bi, gi, idx]
        pos = idx[..., None] * SLC_BLOCK + jnp.arange(SLC_BLOCK)
        dist_s = t[None, None, :, None, None] - pos
        bias_s = tbl[g6, h6, rel_bucket(dist_s)[:, :, None]]
        s_s = jnp.einsum('bqghd,bgqkld->bghqkl', qs, kg).astype(jnp.float32) + bias_s
        s_s = jnp.where((dist_s >= 0)[:, :, None], s_s, NEG)
        shp = s_s.shape
        p_s = jax.nn.softmax(s_s.reshape(shp[:4] + (n_sel * SLC_BLOCK,)), axis=-1).reshape(shp)
        o_slc = jnp.einsum('bghqkl,bgqkld->bqghd', p_s.astype(dt), vg)
        kw = lax.dynamic_slice_in_dim(kw_pad, i * Q_BLOCK, Q_BLOCK + WINDOW, axis=1)
        vw = lax.dynamic_slice_in_dim(vw_pad, i * Q_BLOCK, Q_BLOCK + WINDOW, axis=1)
        kpos = i * Q_BLOCK - WINDOW + jnp.arange(Q_BLOCK + WINDOW)
        dist_w = t[:, None] - kpos[None, :]
        valid_w = (dist_w >= 0) & (dist_w < WINDOW) & (kpos[None, :] >= 0)
        s_w = jnp.einsum('bqghd,bkgd->bghqk', qs, kw).astype(jnp.float32)
        s_w = s_w + tbl[:, :, rel_bucket(dist_w)]
        p_w = masked_softmax(s_w, valid_w)
        o_win = jnp.einsum('bghqk,bkgd->bqghd', p_w.astype(dt), vw)
        o = gb[..., 0:1] * o_cmp + gb[..., 1:2] * o_slc + gb[..., 2:3] * o_win
        return o.reshape(B, Q_BLOCK, H * Dh)

    q_blocks = q.reshape(B, nq, Q_BLOCK, G, HPG, Dh).transpose(1, 0, 2, 3, 4, 5)
    g_blocks = gates.reshape(B, nq, Q_BLOCK, G, HPG, N_BRANCH).transpose(1, 0, 2, 3, 4, 5)
    out = lax.map(block, (jnp.arange(nq, dtype=jnp.int32), q_blocks, g_blocks))
    return out.transpose(1, 0, 2, 3).reshape(B, S, H * Dh)


def conv_ffn(h, w_up, conv_w, conv_b, w_down):
    S = h.shape[1]
    u = h @ w_up
    up = jnp.pad(u, ((0, 0), (CONV_WIDTH - 1, 0), (0, 0)))
    c = conv_b + sum(conv_w[k] * up[:, k:k + S] for k in range(CONV_WIDTH))
    a, b = jnp.split(c, 2, axis=-1)
    return (jax.nn.silu(a) * b) @ w_down


def setup_inputs(seed: int = 0) -> dict:
    key = jax.random.key(seed)
    ks = jax.random.split(key, 20)
    f = jnp.float32

    def nrm(k, shape, s):
        return jax.random.normal(k, shape, f) * s

    L = DEPTH
    return {
        "x": nrm(ks[0], (BATCH, SEQ, D_MODEL), 1.0),
        "norm_mix_g": 1.0 + nrm(ks[1], (L, D_MODEL), 0.02),
        "w_in": nrm(ks[2], (L, D_MODEL, N_IN), D_MODEL ** -0.5),
        "pool_w": nrm(ks[3], (L, len(POOL_WINDOWS), POOL_GROUP, POOL_GROUP), POOL_GROUP ** -0.5),
        "pool_scale": 1.0 + nrm(ks[4], (L, POOL_WIDTH), 0.02),
        "cmp_pe_k": nrm(ks[5], (L, CMP_BLOCK, HEAD_DIM), 0.02),
        "cmp_w1_k": nrm(ks[6], (L, CMP_BLOCK, HEAD_DIM, HEAD_DIM), (CMP_BLOCK * HEAD_DIM) ** -0.5),
        "cmp_w2_k": nrm(ks[7], (L, HEAD_DIM, HEAD_DIM), HEAD_DIM ** -0.5),
        "cmp_pe_v": nrm(ks[8], (L, CMP_BLOCK, HEAD_DIM), 0.02),
        "cmp_w1_v": nrm(ks[9], (L, CMP_BLOCK, HEAD_DIM, HEAD_DIM), (CMP_BLOCK * HEAD_DIM) ** -0.5),
        "cmp_w2_v": nrm(ks[10], (L, HEAD_DIM, HEAD_DIM), HEAD_DIM ** -0.5),
        "rel_bias": nrm(ks[11], (REL_BUCKETS, NSA_HEADS), 0.5),
        "w_out": nrm(ks[12], (L, D_MIX, D_MODEL), D_MIX ** -0.5),
        "norm_ffn_g": 1.0 + nrm(ks[13], (L, D_MODEL), 0.02),
        "w_up": nrm(ks[14], (L, D_MODEL, 2 * D_FF), D_MODEL ** -0.5),
        "conv_w": nrm(ks[15], (L, CONV_WIDTH, 2 * D_FF), CONV_WIDTH ** -0.5),
        "conv_b": nrm(ks[16], (L, 2 * D_FF), 0.02),
        "w_down": nrm(ks[17], (L, D_FF, D_MODEL), D_FF ** -0.5),
        "norm_final_g": 1.0 + nrm(ks[18], (D_MODEL,), 0.02),
    }


def reference(x, norm_mix_g, w_in, pool_w, pool_scale, cmp_pe_k, cmp_w1_k, cmp_w2_k,
              cmp_pe_v, cmp_w1_v, cmp_w2_v, rel_bias, w_out, norm_ffn_g, w_up, conv_w,
              conv_b, w_down, norm_final_g):
    B, S, _ = x.shape
    splits = _in_splits()
    h = x
    for l in range(DEPTH):
        hn = rms_norm(h, norm_mix_g[l])
        proj = hn @ w_in[l]
        u_pool, q, kc, vc, ksl, vsl, kw, vw, gl = jnp.split(proj, splits, axis=-1)
        kvs = lambda t: t.reshape(B, S, KV_GROUPS, HEAD_DIM)
        y_pool = multiscale_pool(u_pool, pool_w[l], pool_scale[l])
        y_nsa = native_sparse_attention(
            q.reshape(B, S, NSA_HEADS, HEAD_DIM), kvs(kc), kvs(vc), kvs(ksl), kvs(vsl),
            kvs(kw), kvs(vw), gl, cmp_pe_k[l], cmp_w1_k[l], cmp_w2_k[l],
            cmp_pe_v[l], cmp_w1_v[l], cmp_w2_v[l], rel_bias)
        h = h + jnp.concatenate([y_nsa, y_pool], axis=-1) @ w_out[l]
        h = h + conv_ffn(rms_norm(h, norm_ffn_g[l]), w_up[l], conv_w[l], conv_b[l], w_down[l])
    return rms_norm(h, norm_final_g)
```

```python
import math
from contextlib import ExitStack

import numpy as np
import ml_dtypes

import concourse.bass as bass
import concourse.mybir as mybir
from concourse.bass_utils import run_bass_kernel_spmd

F32 = mybir.dt.float32
BF16 = mybir.dt.bfloat16
AF = mybir.ActivationFunctionType
ALU = mybir.AluOpType

NCORES = 8
D = 4096
KC = D // 128
S_ALL = 8192
NT = 64
NQT = 9
NQ = NQT * 128
EXT0 = 51
NEXT = (NT - EXT0) * 128
Q0T = 55
HEADS = 24
G = 4
HPG = 6
DFF = 11008
FFC = DFF // 128
NBLK_IN = 57
SCALE = 128 ** -0.5
NEGB = -30000.0
ENGS = ("pe", "act", "dve", "pool", "sp")


class Sched:
    def __init__(self, nc, stack):
        self.nc = nc
        self.stack = stack
        self.q = {e: [] for e in ENGS}
        self.cnt = {e: 0 for e in ENGS}
        self.sem = {e: stack.enter_context(nc.semaphore("s_" + e)) for e in ENGS}
        self.waited = {e: {} for e in ENGS}
        self.lastw = {}
        self.lastr = {}
        self.dsem = {}
        self.dcnt = {}

    def _semobj(self, key):
        return self.sem[key] if key in self.sem else self.dsem[key]

    def _wait(self, eng, k, v):
        w = self.waited[eng]
        if w.get(k, 0) >= v:
            return
        w[k] = v
        so = self._semobj(k)
        self.q[eng].append(lambda E, so=so, v=v: E.wait_ge(so, v))

    def _deps(self, eng, reads, writes):
        deps = {}

        def add(k, v):
            if k == "pe" and eng == "pe":
                return
            if deps.get(k, 0) < v:
                deps[k] = v

        for b in reads:
            if b in self.lastw:
                add(*self.lastw[b])
        for b in writes:
            if b in self.lastw:
                add(*self.lastw[b])
            for k, v in self.lastr.get(b, {}).items():
                add(k, v)
        for k, v in deps.items():
            self._wait(eng, k, v)

    def _mark(self, key, val, reads, writes):
        for b in reads:
            self.lastr.setdefault(b, {})[key] = val
        for b in writes:
            self.lastw[b] = (key, val)
            self.lastr[b] = {}

    def op(self, eng, fn, reads=(), writes=(), inc=True):
        self._deps(eng, reads, writes)
        if inc:
            self.cnt[eng] += 1
            v = self.cnt[eng]
            so = self.sem[eng]
            self.q[eng].append(lambda E, fn=fn, so=so: fn(E).then_inc(so, 1))
        else:
            v = self.cnt[eng] + 1
            self.q[eng].append(lambda E, fn=fn: fn(E))
        self._mark(eng, v, reads, writes)

    def dma(self, qeng, out, in_, reads=(), writes=(), key=None, **kw):
        self._deps(qeng, reads, writes)
        if key not in self.dsem:
            self.dsem[key] = self.stack.enter_context(self.nc.semaphore(key[:30]))
            self.dcnt[key] = 0
        self.dcnt[key] += 16
        v = self.dcnt[key]
        so = self.dsem[key]
        self.q[qeng].append(
            lambda E, out=out, in_=in_, so=so, kw=kw: E.dma_start(out=out, in_=in_, **kw).then_inc(so, 16))
        self._mark(key, v, reads, writes)

    def barrier(self):
        for eng in ENGS:
            for k in list(self.sem) + list(self.dsem):
                v = self.cnt[k] if k in self.sem else self.dcnt[k]
                if v and k != eng:
                    self._wait(eng, k, v)
        self.lastw.clear()
        self.lastr.clear()

    def replay(self):
        nc = self.nc
        with nc.Block() as block:
            @block.tensor
            def _(E):
                for f in self.q["pe"]:
                    f(E)

            @block.scalar
            def _(E):
                for f in self.q["act"]:
                    f(E)

            @block.vector
            def _(E):
                for f in self.q["dve"]:
                    f(E)

            @block.gpsimd
            def _(E):
                for f in self.q["pool"]:
                    f(E)

            @block.sync
            def _(E):
                for f in self.q["sp"]:
                    f(E)


class Ring:
    def __init__(self, name, bufs):
        self.name = name
        self.bufs = bufs
        self.i = -1

    def next(self):
        self.i += 1
        k = self.i % len(self.bufs)
        return self.bufs[k], "%s%d" % (self.name, k)


def build_nc(debug=False, stop_after=99, small_ffn=False, start_at=1, p6_mode=4):
    nc = bass.Bass("TRN2", target_bir_lowering=False)
    st = ExitStack()

    def din(name, shape, dt=F32):
        return nc.dram_tensor(name, list(shape), dt, kind="ExternalInput").ap()

    def dscr(name, shape, dt):
        kind = "ExternalOutput" if (debug and name in DEBUG_OUT) else "Internal"
        return nc.dram_tensor(name, list(shape), dt, kind=kind).ap()

    xf = din("xf", [S_ALL, D])
    win = din("win", [NBLK_IN, 128, KC, 128])
    wout = din("wout", [32, 128, KC, 128])
    wup = din("wup", [1 if small_ffn else 172, 128, KC, 128])
    wdn = din("wdn", [1 if small_ffn else 32, 128, FFC, 128])
    g1 = din("g1", [1, D])
    g2 = din("g2", [1, D])
    g3 = din("g3", [1, D])
    poolw = din("poolw", [128, 4, 2, 256])
    poolsc = din("poolsc", [128, 8])
    convw = din("convw", [128, 172, 3])
    convb = din("convb", [128, 172])
    w1k = din("w1k", [128, 32, 128])
    w2k = din("w2k", [128, 128])
    pekT = din("pekT", [128, 32])
    w1v = din("w1v", [128, 32, 128])
    w2v = din("w2v", [128, 128])
    pevT = din("pevT", [128, 32])
    ba_in = din("ba", [128, HEADS, 2, 128])
    b31_in = din("b31t", [128, HEADS])
    ramask_in = din("ramask", [128, 2, 128])
    bband_in = din("bband", [128, HEADS, 15])
    bbmask_in = din("bbmask", [128, 15])
    ident_in = din("ident", [128, 128], BF16)
    identf_in = din("identf", [128, 128])
    ones_in = din("onesb", [128, 128], BF16)
    wslc_in = din("wslc", [128, 4, 128], BF16)
    wmsk_in = din("wmsk", [128, 7, 384], BF16)
    kbias_in = din("kbias", [128, NT])
    cex_in = din("cex", [1, 512])
    midm_in = din("midmask", [128, NQT, 128])
    cst_in = din("cst", [128, NQT, 128])
    vmask_in = din("vmask", [128, NQT, 128])
    invc_in = din("invcnt", [1, 4 * NQ])

    out = nc.dram_tensor("out", [1024, D], F32, kind="ExternalOutput").ap()

    KslT = dscr("KslT", [G, 128, S_ALL], BF16)
    VslT = dscr("VslT", [G, 128, S_ALL], BF16)
    KcrT = dscr("KcrT", [G, 128, S_ALL], BF16)
    VcrT = dscr("VcrT", [G, 128, S_ALL], BF16)
    QT = dscr("QT", [HEADS, 128, NQ], BF16)
    KwT = dscr("KwT", [G, 128, NEXT], BF16)
    VwT = dscr("VwT", [G, 128, NEXT], BF16)
    UpT = dscr("UpT", [8, 128, NEXT], F32)
    GlT = dscr("GlT", [128, NQ], F32)
    catT = dscr("catT", [32, 128, NQ], BF16)
    hS = dscr("hS", [NQ, D], F32)
    yS = dscr("yS", [1024, D], F32)
    selD = dscr("selD", [G, 128, NQ], BF16)
    if debug:
        dbgKc = nc.dram_tensor("dbgKc", [128, G, 512], BF16, kind="ExternalOutput").ap()
        dbgVc = nc.dram_tensor("dbgVc", [128, G, 4, 128], BF16, kind="ExternalOutput").ap()
        dbgOc = nc.dram_tensor("dbgOc", [128, HPG, NQ], BF16, kind="ExternalOutput").ap()
        dbgAcc = nc.dram_tensor("dbgAcc", [3, 128, 384], F32, kind="ExternalOutput").ap()

    S = Sched(nc, st)

    uid = [0]

    def sb(stack, name, shape, dt):
        uid[0] += 1
        return stack.enter_context(nc.sbuf_tensor("sb%d_%s" % (uid[0], name), list(shape), dt))

    P = [st.enter_context(nc.psum_tensor("P%d" % i, [128, 512], F32)) for i in range(6)]
    PB = [st.enter_context(nc.psum_tensor("PB%d" % i, [128, 1024], BF16)) for i in range(2)]

    ident = sb(st, "ident", [128, 128], BF16)
    identf = sb(st, "identf", [128, 128], F32)
    onesb = sb(st, "onesb", [128, 128], BF16)
    S.dma("sp", ident[:], ident_in, writes=["ident"], key="c0")
    S.dma("sp", identf[:], identf_in, writes=["identf"], key="c1")
    S.dma("sp", onesb[:], ones_in, writes=["onesb"], key="c2")

    def norm_transpose(src_rows, gt, xring, xsring, ssring, hnT, hn_key, col, tsel):
        xt, xk = xring.next()
        S.dma("sp", xt[:], src_rows, reads=[], writes=[xk], key="d_" + xk)
        ss, sk = ssring.next()
        xs, xsk = xsring.next()
        S.op("act", lambda E: E.activation(xs[:], xt[:], AF.Square, accum_out=ss[:, 0:1]),
             reads=[xk], writes=[xsk, sk])
        S.op("dve", lambda E: E.tensor_scalar(ss[:, 1:2], ss[:, 0:1], 1.0 / D, 1e-6, op0=ALU.mult, op1=ALU.add),
             reads=[sk], writes=[sk])
        S.op("act", lambda E: E.activation(ss[:, 2:3], ss[:, 1:2], AF.Sqrt), reads=[sk], writes=[sk])
        S.op("dve", lambda E: E.reciprocal(ss[:, 3:4], ss[:, 2:3]), reads=[sk], writes=[sk])
        S.op("dve", lambda E: E.scalar_tensor_tensor(xs[:], xt[:], ss[:, 3:4], gt[:], op0=ALU.mult, op1=ALU.mult),
             reads=[xk, sk, "gt"], writes=[xsk])
        for q4 in range(4):
            pb = PB[(tsel + q4) % 2]
            pbk = "PB%d" % ((tsel + q4) % 2)
            for j in range(8):
                kc = q4 * 8 + j
                S.op("pe", lambda E, pb=pb, j=j, kc=kc: E.transpose(pb[:, j * 128:(j + 1) * 128],
                                                                   xs[:, kc * 128:(kc + 1) * 128], ident[:]),
                     reads=[xsk, "ident"], writes=[pbk], inc=(j == 7))
            eng = "act" if q4 % 2 == 0 else "dve"
            dst = hnT[:, q4 * 8:(q4 + 1) * 8, col:col + 128]
            srcp = pb[:, :].rearrange("p (a b) -> p a b", a=8)
            if eng == "act":
                S.op("act", lambda E, dst=dst, srcp=srcp: E.copy(dst, srcp), reads=[pbk], writes=[hn_key])
            else:
                S.op("dve", lambda E, dst=dst, srcp=srcp: E.tensor_copy(dst, srcp), reads=[pbk], writes=[hn_key])

    def dense_block(wsrc, kcn, wring, act, act_keys, col_segs, epilogue):
        wt, wk = wring.next()
        for k0 in range(0, kcn, 32):
            k1 = min(kcn, k0 + 32)
            S.dma("pool", wt[:, k0:k1, :], wsrc[:, k0:k1, :], reads=[], writes=[wk], key="d_" + wk)
        for (c0, n, pt, pk, pc) in col_segs:
            for kc in range(kcn):
                S.op("pe", lambda E, pt=pt, pc=pc, n=n, kc=kc, c0=c0: E.matmul(
                    pt[:, pc:pc + n], lhsT=wt[:, kc, :], rhs=act(kc, c0, n), start=(kc == 0), stop=(kc == kcn - 1)),
                    reads=[wk] + act_keys, writes=[pk], inc=(kc == kcn - 1))
        epilogue()

    if stop_after >= 1 and start_at <= 1:
        with ExitStack() as ph:
            hnT = sb(ph, "hnT", [128, KC, 1024], BF16)
            gt = sb(ph, "gt", [128, D], F32)
            xring = Ring("xt", [sb(ph, "xt%d" % i, [128, D], F32) for i in range(2)])
            xsring = Ring("xs", [sb(ph, "xs%d" % i, [128, D], BF16) for i in range(2)])
            ssring = Ring("ss", [sb(ph, "ss%d" % i, [128, 4], F32) for i in range(2)])
            wring = Ring("wb", [sb(ph, "wb%d" % i, [128, KC, 128], BF16) for i in range(4)])
            stg = Ring("stg", [sb(ph, "stg%d" % i, [128, 512], BF16) for i in range(4)])
            stgf = Ring("stgf", [sb(ph, "stgf%d" % i, [128, 512], F32) for i in range(2)])
            S.dma("sp", gt[:], g1.partition_broadcast(128), writes=["gt"], key="c3")
            pcount = 0
            for sti in range(8):
                for tl in range(8):
                    tau = sti * 8 + tl
                    norm_transpose(xf[tau * 128:(tau + 1) * 128, :], gt, xring, xsring, ssring, hnT, "hnT",
                                   tl * 128, tl)
                blocks = list(range(32, 48)) if sti < 6 else list(range(NBLK_IN))
                ft0 = sti * 1024
                for blk in blocks:
                    segs = []
                    for half in range(2):
                        pi = pcount % 6
                        pcount += 1
                        segs.append((half * 512, 512, P[pi], "P%d" % pi, 0))

                    def epi(blk=blk, segs=segs, ft0=ft0):
                        for half, (c0, n, pt, pk, pc) in enumerate(segs):
                            f0 = ft0 + c0
                            if blk < 8:
                                lo = max(f0, EXT0 * 128)
                                if lo >= f0 + 512:
                                    continue
                                sg, sgk = stgf.next()
                                S.op("dve", lambda E, sg=sg, pt=pt: E.tensor_copy(sg[:], pt[:, 0:512]),
                                     reads=[pk], writes=[sgk])
                                S.dma("sp", UpT[blk, :, lo - EXT0 * 128: f0 + 512 - EXT0 * 128],
                                      sg[:, lo - f0:512], reads=[sgk], writes=["UpT"], key="d_" + sgk)
                            elif blk == 56:
                                lo = max(f0, Q0T * 128)
                                if lo >= f0 + 512:
                                    continue
                                sg, sgk = stgf.next()
                                S.op("act", lambda E, sg=sg, pt=pt: E.activation(sg[:], pt[:, 0:512], AF.Sigmoid),
                                     reads=[pk], writes=[sgk])
                                S.dma("sp", GlT[:, lo - Q0T * 128: f0 + 512 - Q0T * 128],
                                      sg[:, lo - f0:512], reads=[sgk], writes=["GlT"], key="d_" + sgk)
                            else:
                                if 8 <= blk < 32:
                                    dst, base = QT[blk - 8], Q0T * 128
                                elif blk < 36:
                                    dst, base = KcrT[blk - 32], 0
                                elif blk < 40:
                                    dst, base = VcrT[blk - 36], 0
                                elif blk < 44:
                                    dst, base = KslT[blk - 40], 0
                                elif blk < 48:
                                    dst, base = VslT[blk - 44], 0
                                elif blk < 52:
                                    dst, base = KwT[blk - 48], EXT0 * 128
                                else:
                                    dst, base = VwT[blk - 52], EXT0 * 128
                                lo = max(f0, base)
                                if lo >= f0 + 512:
                                    continue
                                sg, sgk = stg.next()
                                if half == 0:
                                    S.op("act", lambda E, sg=sg, pt=pt: E.copy(sg[:], pt[:, 0:512]),
                                         reads=[pk], writes=[sgk])
                                else:
                                    S.op("dve", lambda E, sg=sg, pt=pt: E.tensor_copy(sg[:], pt[:, 0:512]),
                                         reads=[pk], writes=[sgk])
                                S.dma("sp", dst[:, lo - base: f0 + 512 - base], sg[:, lo - f0:512],
                                      reads=[sgk], writes=["scr"], key="d_" + sgk)

                    dense_block(win[blk], KC, wring,
                                lambda kc, c0, n: hnT[:, kc, c0:c0 + n], ["hnT"], segs, epi)
            S.barrier()

    if stop_after >= 2 and start_at <= 2:
        with ExitStack() as ph:
            wslc = sb(ph, "wslc", [128, 4, 128], BF16)
            wmsk = sb(ph, "wmsk", [128, 7, 384], BF16)
            kbias = sb(ph, "kbias", [128, NT], F32)
            ball = sb(ph, "ball", [128, HEADS, NT], F32)
            b31 = sb(ph, "b31", [128, HEADS], F32)
            nb31 = sb(ph, "nb31", [128, HEADS], F32)
            RA = sb(ph, "RA", [128, HEADS, 2, 128], BF16)
            ratmp = sb(ph, "ratmp", [128, 2, 128], F32)
            ramask = sb(ph, "ramask", [128, 2, 128], F32)
            RB = sb(ph, "RB", [128, HEADS, 15], F32)
            bbmask = sb(ph, "bbmask", [128, 15], F32)
            cex = sb(ph, "cex", [128, 512], F32)
            KcT = sb(ph, "KcT", [128, G, 512], BF16)
            Vc = sb(ph, "Vc", [128, G, 4, 128], BF16)
            S.dma("sp", wslc[:], wslc_in, writes=["wslc"], key="c1")
            S.dma("sp", wmsk[:], wmsk_in, writes=["wmsk"], key="c2")
            S.dma("sp", kbias[:], kbias_in, writes=["kbias"], key="c3")
            S.dma("sp", b31[:], b31_in, writes=["b31"], key="c4")
            S.dma("sp", ramask[:], ramask_in, writes=["ramask"], key="c5")
            S.dma("sp", bbmask[:], bbmask_in, writes=["bbmask"], key="c6")
            S.dma("sp", cex[:], cex_in.partition_broadcast(128), writes=["cex"], key="c7")
            S.dma("sp", RB[:], bband_in, writes=["RB"], key="c11")
            for h in range(HEADS):
                S.op("dve", lambda E, h=h: E.tensor_scalar(ball[:, h, :], kbias[:], b31[:, h:h + 1], None, op0=ALU.add),
                     reads=["kbias", "b31"], writes=["ball"])
            S.op("dve", lambda E: E.tensor_scalar(nb31[:], b31[:], -1.0, None, op0=ALU.mult), reads=["b31"], writes=["nb31"])
            for h in range(HEADS):
                S.dma("sp", ratmp[:], ba_in[:, h], writes=["ratmp"], key="c12")
                S.op("act", lambda E, h=h: E.activation(ratmp[:], ratmp[:], AF.Exp, bias=nb31[:, h:h + 1]),
                     reads=["ratmp", "nb31"], writes=["ratmp"])
                S.op("dve", lambda E, h=h: E.tensor_tensor(RA[:, h], ratmp[:], ramask[:], op=ALU.mult),
                     reads=["ratmp", "ramask"], writes=["RA"])
                S.op("act", lambda E, h=h: E.activation(RB[:, h, :], RB[:, h, :], AF.Exp, bias=nb31[:, h:h + 1]),
                     reads=["RB", "nb31"], writes=["RB"])
                S.op("dve", lambda E, h=h: E.tensor_tensor(RB[:, h, :], RB[:, h, :], bbmask[:], op=ALU.mult),
                     reads=["RB", "bbmask"], writes=["RB"])

            with ExitStack() as p2:
                w1 = [sb(p2, "w1_%d" % i, [128, 32, 128], BF16) for i in range(2)]
                w2 = [sb(p2, "w2_%d" % i, [128, 128], BF16) for i in range(2)]
                peT = [sb(p2, "peT%d" % i, [128, 32], BF16) for i in range(2)]
                cb = sb(p2, "cbias", [128, 2], F32)
                raw = Ring("raw", [sb(p2, "raw%d" % i, [128, S_ALL + 32], BF16) for i in range(2)])
                hT = Ring("hT", [sb(p2, "hT%d" % i, [128, 512], BF16) for i in range(2)])
                for i, (a_, b_, c_) in enumerate(((w1k, w2k, pekT), (w1v, w2v, pevT))):
                    S.dma("pool", w1[i][:], a_, writes=["w1_%d" % i], key="c13")
                    S.dma("pool", w2[i][:], b_, writes=["w2_%d" % i], key="c14")
                    S.dma("pool", peT[i][:], c_, writes=["peT%d" % i], key="c15")
                    for l in range(32):
                        S.op("pe", lambda E, i=i, l=l: E.matmul(P[0][:, i:i + 1], lhsT=w1[i][:, l, :], rhs=peT[i][:, l:l + 1],
                                                                start=(l == 0), stop=(l == 31)),
                             reads=["w1_%d" % i, "peT%d" % i], writes=["P0"])
                S.op("dve", lambda E: E.tensor_copy(cb[:], P[0][:, 0:2]), reads=["P0"], writes=["cb"])
                for ri, r in enumerate(raw.bufs):
                    S.op("pool", lambda E, r=r: E.memset(r[:, S_ALL:S_ALL + 32], 0.0), writes=["raw%d" % ri])
                for g in range(G):
                    for i, src in enumerate((KcrT, VcrT)):
                        rt, rk = raw.next()
                        S.dma("sp", rt[:, 0:S_ALL], src[g], reads=["scr"], writes=[rk], key="d_" + rk)
                        for l in range(32):
                            S.op("pe", lambda E, i=i, l=l, rt=rt: E.matmul(
                                P[1][:, 0:512], lhsT=w1[i][:, l, :], rhs=rt[:, l:l + 16 * 512:16],
                                start=(l == 0), stop=(l == 31)), reads=[rk, "w1_%d" % i], writes=["P1"], inc=(l == 31))
                        ht, hk = hT.next()
                        S.op("act", lambda E, ht=ht, i=i: E.activation(ht[:], P[1][:, 0:512], AF.Silu, bias=cb[:, i:i + 1]),
                             reads=["P1", "cb"], writes=[hk])
                        if i == 0:
                            S.op("pe", lambda E, ht=ht: E.matmul(P[2][:, 0:512], lhsT=w2[0][:], rhs=ht[:], start=True, stop=True),
                                 reads=[hk, "w2_0"], writes=["P2"])
                            S.op("dve", lambda E, g=g: E.tensor_copy(KcT[:, g, :], P[2][:, 0:512]), reads=["P2"], writes=["KcT"])
                        else:
                            for nt in range(4):
                                S.op("pe", lambda E, ht=ht, nt=nt: E.matmul(P[3][:, nt * 128:(nt + 1) * 128],
                                                                            lhsT=ht[:, nt * 128:(nt + 1) * 128], rhs=w2[1][:],
                                                                            start=True, stop=True),
                                     reads=[hk, "w2_1"], writes=["P3"])
                            S.op("dve", lambda E, g=g: E.tensor_copy(Vc[:, g].rearrange("p a b -> p (a b)"), P[3][:, 0:512]),
                                 reads=["P3"], writes=["Vc"])
                S.barrier()
            if debug:
                S.dma("sp", dbgKc, KcT[:], reads=["KcT"], key="dbg0")
                S.dma("sp", dbgVc, Vc[:], reads=["Vc"], key="dbg0")

            KT = sb(ph, "KT", [128, S_ALL], BF16)
            Vt = sb(ph, "Vt", [128, NT, 128], BF16)
            Qg = sb(ph, "Qg", [128, HPG, NQ], BF16)
            KwTg = sb(ph, "KwTg", [128, NEXT], BF16)
            Vw = sb(ph, "Vw", [128, 13, 128], BF16)
            selT = sb(ph, "selT", [128, NQ], BF16)
            Msk = sb(ph, "Msk", [128, 64, 384], BF16)
            VTs = Msk[:].rearrange("p a b -> p (a b)")
            ocmp = sb(ph, "ocmp", [128, HPG, NQ], BF16)
            g0r = Ring("g0", [sb(ph, "g0_%d" % i, [128, NQ], F32) for i in range(2)])
            gbr = Ring("gb", [sb(ph, "gb_%d" % i, [128, 2, 384], F32) for i in range(2)])
            accr = Ring("acc", [sb(ph, "acc%d" % i, [128, 384], F32) for i in range(2)])
            obr = Ring("ob", [sb(ph, "ob%d" % i, [128, 384], BF16) for i in range(2)])
            Ering = Ring("E", [sb(ph, "E%d" % i, [128, 512], F32) for i in range(2)])
            Pnring = Ring("Pn", [sb(ph, "Pn%d" % i, [128, 512], BF16) for i in range(2)])
            PnTring = Ring("PnT", [sb(ph, "PnT%d" % i, [128, 4, 128], BF16) for i in range(2)])
            rsring = Ring("rs", [sb(ph, "rs%d" % i, [128, 4], F32) for i in range(2)])
            PTring = Ring("PT", [sb(ph, "PT%d" % i, [128, 384], BF16) for i in range(3)])
            wtmp = Ring("wt", [sb(ph, "wt%d" % i, [128, 384], F32) for i in range(2)])
            otmp = Ring("ot", [sb(ph, "ot%d" % i, [128, 384], F32) for i in range(2)])
            mring = Ring("mm", [sb(ph, "mm%d" % i, [128, 3, 128], F32) for i in range(2)])
            sc = sb(ph, "sc", [128, 128], F32)
            sc2 = sb(ph, "sc2", [128, 128], F32)
            mx = sb(ph, "mx", [128, 8], F32)
            selq = sb(ph, "selq", [128, 128], BF16)
            for pn in Pnring.bufs:
                S.op("pool", lambda E, pn=pn: E.memset(pn[:], 0.0), writes=["Pn0", "Pn1"])
            sctr = [0]

            def spsum():
                sctr[0] += 1
                i = sctr[0] % 2
                return P[i], "P%d" % i

            def sT_branch(h, hp, gb, gbk, br, acc, acck, first_branch, Ksrc, Kkey, kcol0, Vsrc, Vkey, vslot0,
                          slots, cb0, mask_of, qtiles):
                for si, tk in enumerate(slots):
                    pt, pk = spsum()
                    S.op("pe", lambda E, pt=pt, tk=tk: E.matmul(
                        pt[:, 0:384], lhsT=Ksrc[:, (tk - kcol0) * 128:(tk - kcol0 + 1) * 128],
                        rhs=Qg[:, hp, cb0:cb0 + 384], start=True, stop=True),
                        reads=[Kkey, "Qg"], writes=[pk])
                    PT, ptk = PTring.next()
                    S.op("act", lambda E, PT=PT, pt=pt, tk=tk: E.activation(
                        PT[:], pt[:, 0:384], AF.Exp, bias=ball[:, h, tk:tk + 1], scale=SCALE),
                        reads=[pk, "ball"], writes=[ptk])
                    m, mk = mask_of(tk)
                    S.op("dve", lambda E, PT=PT, m=m: E.tensor_tensor(PT[:], PT[:], m, op=ALU.mult),
                         reads=[ptk, mk], writes=[ptk])
                    for ii, tq in enumerate(qtiles):
                        if tk == tq or tk == tq - 1:
                            ab = 0 if tk == tq else 1
                            S.op("dve", lambda E, PT=PT, ii=ii, ab=ab: E.tensor_tensor(
                                PT[:, ii * 128:(ii + 1) * 128], PT[:, ii * 128:(ii + 1) * 128], RA[:, h, ab, :], op=ALU.mult),
                                reads=[ptk, "RA"], writes=[ptk])
                    first, last = si == 0, si == len(slots) - 1
                    S.op("pe", lambda E, PT=PT, tk=tk, first=first, last=last: E.matmul(
                        P[2][:, 0:384], lhsT=Vsrc[:, tk - vslot0, :], rhs=PT[:], start=first, stop=last),
                        reads=[ptk, Vkey], writes=["P2"], inc=last)
                    S.op("pe", lambda E, PT=PT, first=first, last=last: E.matmul(
                        P[3][:, 0:384], lhsT=onesb[:], rhs=PT[:], start=first, stop=last),
                        reads=[ptk, "onesb"], writes=["P3"], inc=last)
                wt_, wk_ = wtmp.next()
                S.op("dve", lambda E, wt_=wt_: E.tensor_scalar(wt_[:], P[3][:, 0:384], 1e-30, None, op0=ALU.add),
                     reads=["P3"], writes=[wk_])
                S.op("dve", lambda E, wt_=wt_: E.reciprocal(wt_[:], wt_[:]), reads=[wk_], writes=[wk_])
                S.op("dve", lambda E, wt_=wt_: E.tensor_tensor(wt_[:], wt_[:], gb[:, br - 1, :], op=ALU.mult),
                     reads=[wk_, gbk], writes=[wk_])
                if first_branch:
                    S.op("dve", lambda E, wt_=wt_: E.tensor_tensor(acc[:], P[2][:, 0:384], wt_[:], op=ALU.mult),
                         reads=["P2", wk_], writes=[acck])
                else:
                    ot_, ok_ = otmp.next()
                    S.op("dve", lambda E, wt_=wt_, ot_=ot_: E.tensor_tensor(ot_[:], P[2][:, 0:384], wt_[:], op=ALU.mult),
                         reads=["P2", wk_], writes=[ok_])
                    S.op("pool", lambda E, ot_=ot_: E.tensor_tensor(acc[:], acc[:], ot_[:], op=ALU.add),
                         reads=[ok_, acck], writes=[acck])

            def slc_psum(i):
                if i == 8:
                    return P[3], "P3", 0
                return (P[4], "P4", (i % 4) * 128) if i < 4 else (P[5], "P5", (i % 4) * 128)

            for g in range(G):
                S.dma("sp", KT[:], KslT[g], reads=["scr"], writes=["KT"], key="dKT")
                S.dma("sp", VTs[:, 0:S_ALL], VslT[g], reads=["scr"], writes=["Msk"] + ["Msk%d" % t_ for t_ in range(64)], key="dVTs")
                for hp in range(HPG):
                    S.dma("sp", Qg[:, hp, :], QT[g * HPG + hp], reads=["scr"], writes=["Qg"], key="dQg")
                S.dma("sp", KwTg[:], KwT[g], reads=["scr"], writes=["KwTg"], key="dKw")
                for tau in range(NT):
                    pb, pbk = PB[tau % 2], "PB%d" % (tau % 2)
                    S.op("pe", lambda E, pb=pb, tau=tau: E.transpose(pb[:, 0:128], VTs[:, tau * 128:(tau + 1) * 128], ident[:]),
                         reads=["Msk", "ident"], writes=[pbk])
                    if tau % 2 == 0:
                        S.op("act", lambda E, pb=pb, tau=tau: E.copy(Vt[:, tau, :], pb[:, 0:128]), reads=[pbk], writes=["Vt"])
                    else:
                        S.op("dve", lambda E, pb=pb, tau=tau: E.tensor_copy(Vt[:, tau, :], pb[:, 0:128]), reads=[pbk], writes=["Vt"])
                S.dma("sp", VTs[:, 0:NEXT], VwT[g], reads=["scr"], writes=["Msk"], key="dVTs")
                for t13 in range(13):
                    pb, pbk = PB[t13 % 2], "PB%d" % (t13 % 2)
                    S.op("pe", lambda E, pb=pb, t13=t13: E.transpose(pb[:, 0:128], VTs[:, t13 * 128:(t13 + 1) * 128], ident[:]),
                         reads=["Msk", "ident"], writes=[pbk])
                    S.op("act", lambda E, pb=pb, t13=t13: E.copy(Vw[:, t13, :], pb[:, 0:128]), reads=[pbk], writes=["Vw"])

                for hp in range(HPG):
                    h = g * HPG + hp
                    g0, g0k = g0r.next()
                    S.dma("sp", g0[:], GlT[3 * h:3 * h + 1, :].partition_broadcast(128), reads=["scr"], writes=[g0k], key="d_" + g0k)
                    for i in range(NQT):
                        tau = Q0T + i
                        ncol = 8 * tau + 7
                        pt, pk = spsum()
                        S.op("pe", lambda E, pt=pt, i=i, g=g, hp=hp: E.matmul(
                            pt[:, 0:512], lhsT=Qg[:, hp, i * 128:(i + 1) * 128], rhs=KcT[:, g, :], start=True, stop=True),
                            reads=["Qg", "KcT"], writes=[pk])
                        Et, ek = Ering.next()
                        S.op("act", lambda E, Et=Et, pt=pt, h=h, ncol=ncol: E.activation(
                            Et[:, 0:ncol], pt[:, 0:ncol], AF.Exp, bias=b31[:, h:h + 1], scale=SCALE),
                            reads=[pk, "b31"], writes=[ek])
                        S.op("dve", lambda E, Et=Et, ncol=ncol, h=h: E.tensor_tensor(
                            Et[:, ncol - 15:ncol], Et[:, ncol - 15:ncol], RB[:, h, :], op=ALU.mult),
                            reads=[ek, "RB"], writes=[ek])
                        rs, rk = rsring.next()
                        S.op("dve", lambda E, Et=Et, ncol=ncol: E.tensor_tensor(
                            Et[:, 0:ncol], Et[:, 0:ncol], cex[:, 0:ncol], op=ALU.mult), reads=[ek, "cex"], writes=[ek])
                        S.op("dve", lambda E, Et=Et, rs=rs, ncol=ncol: E.reduce_sum(
                            rs[:, 0:1], Et[:, 0:ncol], axis=mybir.AxisListType.X), reads=[ek], writes=[rk])
                        S.op("dve", lambda E, rs=rs: E.tensor_scalar(rs[:, 1:2], rs[:, 0:1], 1e-30, None, op0=ALU.add),
                             reads=[rk], writes=[rk])
                        S.op("dve", lambda E, rs=rs: E.reciprocal(rs[:, 2:3], rs[:, 1:2]), reads=[rk], writes=[rk])
                        Pn, pnk = Pnring.next()
                        S.op("dve", lambda E, Pn=Pn, Et=Et, rs=rs, ncol=ncol: E.tensor_scalar(
                            Pn[:, 0:ncol], Et[:, 0:ncol], rs[:, 2:3], None, op0=ALU.mult), reads=[ek, rk], writes=[pnk])
                        if ncol < 512:
                            S.op("pool", lambda E, Pn=Pn, ncol=ncol: E.memset(Pn[:, ncol:512], 0.0), writes=[pnk])
                        pb, pbk = PB[i % 2], "PB%d" % (i % 2)
                        for nt in range(4):
                            S.op("pe", lambda E, pb=pb, Pn=Pn, nt=nt: E.transpose(
                                pb[:, nt * 128:(nt + 1) * 128], Pn[:, nt * 128:(nt + 1) * 128], ident[:]),
                                reads=[pnk, "ident"], writes=[pbk], inc=(nt == 3))
                        PnT, pntk = PnTring.next()
                        S.op("act", lambda E, PnT=PnT, pb=pb: E.copy(PnT[:].rearrange("p a b -> p (a b)"), pb[:, 0:512]),
                             reads=[pbk], writes=[pntk])
                        for nt in range(4):
                            S.op("pe", lambda E, PnT=PnT, nt=nt, g=g: E.matmul(
                                P[2][:, 0:128], lhsT=Vc[:, g, nt, :], rhs=PnT[:, nt, :], start=(nt == 0), stop=(nt == 3)),
                                reads=[pntk, "Vc"], writes=["P2"], inc=(nt == 3))
                        S.op("dve", lambda E, i=i, hp=hp, g0=g0: E.tensor_tensor(
                            ocmp[:, hp, i * 128:(i + 1) * 128], P[2][:, 0:128], g0[:, i * 128:(i + 1) * 128], op=ALU.mult),
                            reads=["P2", g0k], writes=["ocmp"])
                        pslc, pslk, col = slc_psum(i)
                        for nt in range(4):
                            S.op("pe", lambda E, PnT=PnT, nt=nt, pslc=pslc, col=col, hp=hp: E.matmul(
                                pslc[:, col:col + 128], lhsT=PnT[:, nt, :], rhs=wslc[:, nt, :],
                                start=(hp == 0 and nt == 0), stop=(hp == HPG - 1 and nt == 3)),
                                reads=[pntk, "wslc"], writes=[pslk], inc=(nt == 3))
                for i in range(NQT):
                    pslc, pslk, col = slc_psum(i)
                    mm_, mmk = mring.next()
                    S.dma("sp", mm_[:, 0, :], midm_in[:, i, :], writes=[mmk], key="d_" + mmk)
                    S.dma("sp", mm_[:, 1, :], cst_in[:, i, :], writes=[mmk], key="d_" + mmk)
                    S.dma("sp", mm_[:, 2, :], vmask_in[:, i, :], writes=[mmk], key="d_" + mmk)
                    S.op("dve", lambda E, pslc=pslc, col=col, mm_=mm_: E.tensor_tensor(sc[:], pslc[:, col:col + 128], mm_[:, 0, :], op=ALU.mult),
                         reads=[pslk, mmk], writes=["sc"])
                    S.op("dve", lambda E, mm_=mm_: E.tensor_tensor(sc[:], sc[:], mm_[:, 1, :], op=ALU.add),
                         reads=["sc", mmk], writes=["sc"])
                    S.op("dve", lambda E: E.max(out=mx[:], in_=sc[:]), reads=["sc"], writes=["mx"])
                    S.op("dve", lambda E: E.match_replace(out=sc2[:], in_to_replace=mx[:], in_values=sc[:], imm_value=-3e9),
                         reads=["sc", "mx"], writes=["sc2"])
                    S.op("dve", lambda E: E.max(out=mx[:], in_=sc2[:]), reads=["sc2"], writes=["mx"])
                    S.op("dve", lambda E: E.tensor_tensor(sc2[:], sc[:], mx[:, 7:8].to_broadcast([128, 128]), op=ALU.is_ge),
                         reads=["sc", "mx"], writes=["sc2"])
                    S.op("dve", lambda E, mm_=mm_: E.tensor_tensor(selq[:], sc2[:], mm_[:, 2, :], op=ALU.mult),
                         reads=["sc2", mmk], writes=["selq"])
                    S.op("pe", lambda E: E.transpose(PB[0][:, 0:128], selq[:], ident[:]), reads=["selq", "ident"], writes=["PB0"])
                    S.op("act", lambda E, i=i: E.copy(selT[:, i * 128:(i + 1) * 128], PB[0][:, 0:128]), reads=["PB0"], writes=["selT"])
                S.dma("sp", selD[g], selT[:], reads=["selT"], writes=["selD"], key="dselD")
                if debug and g == 0:
                    S.dma("sp", dbgOc, ocmp[:], reads=["ocmp"], key="dbg0")

                for b in range(3):
                    cb0 = b * 384
                    qtiles = [Q0T + 3 * b + ii for ii in range(3)]
                    nsl = qtiles[-1] + 1
                    S._wait("sp", "pe", S.cnt["pe"])
                    for tk in range(nsl):
                        for c2 in range(2):
                            S.dma("sp", Msk[64 * c2:64 * c2 + 64, tk, :],
                                  selD[g, 2 * tk + c2:2 * tk + c2 + 1, cb0:cb0 + 384].partition_broadcast(64),
                                  reads=["selD"], writes=["Msk%d" % tk], key="dM%d" % (tk % 8))
                    for tk in range(nsl):
                        kk_ = "dM%d" % (tk % 8)
                        S.lastw["Msk%d" % tk] = (kk_, S.dcnt[kk_])
                    for hp in range(HPG):
                        h = g * HPG + hp
                        gb, gbk = gbr.next()
                        for br in (1, 2):
                            S.dma("sp", gb[:, br - 1, :], GlT[3 * h + br:3 * h + br + 1, cb0:cb0 + 384].partition_broadcast(128),
                                  reads=["scr"], writes=[gbk], key="d_" + gbk)
                        acc, acck = accr.next()
                        sT_branch(h, hp, gb, gbk, 1, acc, acck, True, KT, "KT", 0, Vt, "Vt", 0, list(range(nsl)), cb0,
                                  lambda tk: (Msk[:, tk, :], "Msk%d" % tk), qtiles)
                        wsl = list(range(qtiles[0] - 4, qtiles[-1] + 1))
                        sT_branch(h, hp, gb, gbk, 2, acc, acck, False, KwTg, "KwTg", EXT0, Vw, "Vw", EXT0, wsl, cb0,
                                  lambda tk, q0=qtiles[0]: (wmsk[:, tk - (q0 - 4), :], "wmsk"), qtiles)
                        if debug and g == 0 and hp == 0:
                            S.dma("sp", dbgAcc[b], acc[:], reads=[acck], key="dbg0")
                        ob, obk = obr.next()
                        S.op("dve", lambda E, ob=ob, acc=acc, hp=hp, cb0=cb0: E.tensor_tensor(ob[:], acc[:], ocmp[:, hp, cb0:cb0 + 384], op=ALU.add),
                             reads=[acck, "ocmp"], writes=[obk])
                        S.dma("sp", catT[h, :, cb0:cb0 + 384], ob[:], reads=[obk], writes=["catT"], key="d_" + obk)
            S.barrier()

    if stop_after >= 4 and start_at <= 4:
        with ExitStack() as ph:
            NP = NQ + 16
            u = [sb(ph, "u%d" % i, [128, NP], F32) for i in range(2)]
            t1 = sb(ph, "t1", [128, NP], F32)
            t2 = sb(ph, "t2", [128, NP], F32)
            dT = sb(ph, "dT", [128, 8, NQ], BF16)
            pw = sb(ph, "pw", [128, 4, 2, 256], BF16)
            psc = sb(ph, "psc", [128, 8], F32)
            invc = sb(ph, "invc", [128, 4 * NQ], F32)
            yb = Ring("yb", [sb(ph, "yb%d" % i, [128, NQ], BF16) for i in range(2)])
            S.dma("pool", pw[:], poolw, writes=["pw"], key="c0")
            S.dma("sp", psc[:], poolsc, writes=["psc"], key="c1")
            S.dma("sp", invc[:], invc_in.partition_broadcast(128), writes=["invc"], key="c2")
            off = (Q0T - EXT0) * 128 - 16
            for ch in range(8):
                gi = ch // 2
                w = (2, 4, 8, 16)[gi]
                ut, uk = u[ch % 2], "u%d" % (ch % 2)
                S.dma("sp", ut[:], UpT[ch, :, off:off + NP], reads=["scr"], writes=[uk], key="d_" + uk)
                cur, curk = ut, uk
                sh = 1
                bufs = [(t1, "t1"), (t2, "t2")]
                bi = 0
                while sh < w:
                    nx, nxk = bufs[bi % 2]
                    bi += 1
                    S.op("dve", lambda E, nx=nx, cur=cur, sh=sh: E.tensor_tensor(
                        nx[:, 2 * sh - 1:NP], cur[:, 2 * sh - 1:NP], cur[:, sh - 1:NP - sh], op=ALU.add), reads=[curk], writes=[nxk])
                    cur, curk = nx, nxk
                    sh *= 2
                S.op("dve", lambda E, cur=cur, gi=gi: E.tensor_tensor(cur[:, 16:NP], cur[:, 16:NP], invc[:, gi * NQ:(gi + 1) * NQ], op=ALU.mult),
                     reads=[curk, "invc"], writes=[curk])
                S.op("dve", lambda E, cur=cur, ut=ut, ch=ch: E.tensor_tensor(dT[:, ch, :], cur[:, 16:NP], ut[:, 16:NP], op=ALU.subtract),
                     reads=[curk, uk], writes=["dT"])
            for gi in range(4):
                for ec in range(2):
                    ych = gi * 2 + ec
                    ybt, ybk = yb.next()
                    for b in range(3):
                        pi = (ych * 3 + b) % 4
                        for cc in range(2):
                            S.op("pe", lambda E, pi=pi, gi=gi, cc=cc, ec=ec, b=b: E.matmul(
                                P[pi][:, 0:384], lhsT=pw[:, gi, cc, ec * 128:(ec + 1) * 128],
                                rhs=dT[:, gi * 2 + cc, b * 384:(b + 1) * 384], start=(cc == 0), stop=(cc == 1)),
                                reads=["pw", "dT"], writes=["P%d" % pi])
                        S.op("dve", lambda E, pi=pi, ybt=ybt, b=b, ych=ych: E.tensor_scalar(
                            ybt[:, b * 384:(b + 1) * 384], P[pi][:, 0:384], psc[:, ych:ych + 1], None, op0=ALU.mult),
                            reads=["P%d" % pi, "psc"], writes=[ybk])
                    S.dma("sp", catT[24 + ych], ybt[:], reads=[ybk], writes=["catT"], key="d_" + ybk)
            S.barrier()

    if stop_after >= 5 and start_at <= 5:
        with ExitStack() as ph:
            cat = sb(ph, "cat", [128, 32, NQ], BF16)
            wring = Ring("wo", [sb(ph, "wo%d" % i, [128, KC, 128], BF16) for i in range(3)])
            hTs = Ring("hTs", [sb(ph, "hTs%d" % i, [128, NQ], F32) for i in range(2)])
            xq = Ring("xq", [sb(ph, "xq%d" % i, [128, NQT, 128], F32) for i in range(2)])
            hq = Ring("hq", [sb(ph, "hq%d" % i, [128, NQT, 128], F32) for i in range(2)])
            for c in range(32):
                S.dma("sp", cat[:, c, :], catT[c], reads=["catT"], writes=["cat"], key="dcat")
            xq_src = xf[Q0T * 128:NT * 128, :].rearrange("(t p) e -> p t e", p=128)
            hq_dst = hS.rearrange("(t p) e -> p t e", p=128)
            for eb in range(32):
                segs = [(b * 384, 384, P[(eb * 3 + b) % 4], "P%d" % ((eb * 3 + b) % 4), 0) for b in range(3)]
                hT_, hTk = hTs.next()

                def epi(eb=eb, segs=segs, hT_=hT_, hTk=hTk):
                    for b, (c0, n, pt, pk, pc) in enumerate(segs):
                        if b % 2 == 0:
                            S.op("act", lambda E, pt=pt, c0=c0: E.copy(hT_[:, c0:c0 + 384], pt[:, 0:384]), reads=[pk], writes=[hTk])
                        else:
                            S.op("dve", lambda E, pt=pt, c0=c0: E.tensor_copy(hT_[:, c0:c0 + 384], pt[:, 0:384]), reads=[pk], writes=[hTk])
                    xt_, xk_ = xq.next()
                    S.dma("sp", xt_[:], xq_src[:, :, eb * 128:(eb + 1) * 128], writes=[xk_], key="d_" + xk_)
                    ht_, hk_ = hq.next()
                    for t in range(NQT):
                        pj = 4 + (t % 2)
                        S.op("pe", lambda E, t=t, pj=pj: E.transpose(P[pj][:, 0:128], hT_[:, t * 128:(t + 1) * 128], identf[:]),
                             reads=[hTk, "identf"], writes=["P%d" % pj])
                        S.op("dve", lambda E, t=t, pj=pj, ht_=ht_, xt_=xt_: E.tensor_tensor(ht_[:, t, :], P[pj][:, 0:128], xt_[:, t, :], op=ALU.add),
                             reads=["P%d" % pj, xk_], writes=[hk_])
                    S.dma("sp", hq_dst[:, :, eb * 128:(eb + 1) * 128], ht_[:], reads=[hk_], writes=["hS"], key="d_" + hk_)

                dense_block(wout[eb], KC, wring, lambda kc, c0, n: cat[:, kc, c0:c0 + n], ["cat"], segs, epi)
            S.barrier()

    if stop_after >= 6:
        with ExitStack() as ph:
            FH = FFC // 2
            hn2 = sb(ph, "hn2", [128, KC, 640], BF16)
            gt = sb(ph, "gt2", [128, D], F32)
            actT = sb(ph, "actT", [128, FH, 512], BF16)
            cw = sb(ph, "cw", [128, 172, 3], F32)
            cbv = sb(ph, "cbv", [128, 172], F32)
            wring = Ring("wf", [sb(ph, "wf%d" % i, [128, FH, 128], BF16) for i in range(3)])
            xring = Ring("xt", [sb(ph, "fxt%d" % i, [128, D], F32) for i in range(2)])
            xsring = Ring("xs", [sb(ph, "fxs%d" % i, [128, D], BF16) for i in range(1)])
            ssring = Ring("ss", [sb(ph, "fss%d" % i, [128, 4], F32) for i in range(2)])
            ub = Ring("ub", [sb(ph, "ub%d" % i, [128, 514], F32) for i in range(2)])
            ca = Ring("ca", [sb(ph, "ca%d" % i, [128, 512], F32) for i in range(2)])
            asl = sb(ph, "asl", [128, 512], F32)
            yT = Ring("yT", [sb(ph, "yT%d" % i, [128, 512], F32) for i in range(2)])
            yq = Ring("yq", [sb(ph, "yq%d" % i, [128, 4, 128], F32) for i in range(2)])
            yp = Ring("yp", [sb(ph, "yp%d" % i, [128, 4, 128], F32) for i in range(2)])
            S.dma("sp", cw[:], convw, writes=["cw"], key="c2")
            S.dma("sp", cbv[:], convb, writes=["cbv"], key="c3")
            hctr = 0
            pcount = 0
            for half in range(2):
                S.dma("sp", gt[:], g2.partition_broadcast(128), writes=["gt"], key="c0")
                for tl in range(5):
                    tq = half * 4 + tl
                    norm_transpose(hS[tq * 128:(tq + 1) * 128, :], gt, xring, xsring, ssring, hn2, "hn2", tl * 128, tl)
                ydst = yS[half * 512:(half + 1) * 512, :].rearrange("(t p) e -> p t e", p=128)
                for fh in range(2 if p6_mode >= 2 else 0):
                    for jl in range(FH):
                        j = fh * FH + jl
                        for ab in range(2):
                            blk = j + ab * FFC
                            pm = P[pcount % 4]
                            pmk = "P%d" % (pcount % 4)
                            pcount += 1
                            hc = (hctr % 8) * 8
                            hk_ = "P4h%d" % (hctr % 8)
                            hctr += 1
                            segs = [(128, 512, pm, pmk, 0), (126, 2, P[4], hk_, hc)]

                            def epi(jl=jl, ab=ab, blk=blk, pm=pm, pmk=pmk, hc=hc, hk_=hk_):
                                ut, uk = ub.next()
                                S.op("act", lambda E, ut=ut, hc=hc: E.copy(ut[:, 0:2], P[4][:, hc:hc + 2]), reads=[hk_], writes=[uk])
                                S.op("act", lambda E, ut=ut, pm=pm: E.copy(ut[:, 2:514], pm[:, 0:512]), reads=[pmk], writes=[uk])
                                ct, ck = ca.next()
                                S.op("dve", lambda E, ut=ut, ct=ct, blk=blk: E.tensor_scalar(
                                    ct[:], ut[:, 2:514], cw[:, blk, 2:3], cbv[:, blk:blk + 1], op0=ALU.mult, op1=ALU.add),
                                    reads=[uk, "cw", "cbv"], writes=[ck])
                                S.op("dve", lambda E, ut=ut, ct=ct, blk=blk: E.scalar_tensor_tensor(
                                    ct[:], ut[:, 1:513], cw[:, blk, 1:2], ct[:], op0=ALU.mult, op1=ALU.add),
                                    reads=[uk, "cw", ck], writes=[ck])
                                S.op("dve", lambda E, ut=ut, ct=ct, blk=blk: E.scalar_tensor_tensor(
                                    ct[:], ut[:, 0:512], cw[:, blk, 0:1], ct[:], op0=ALU.mult, op1=ALU.add),
                                    reads=[uk, "cw", ck], writes=[ck])
                                if ab == 0:
                                    S.op("act", lambda E, ct=ct: E.activation(asl[:], ct[:], AF.Silu), reads=[ck], writes=["asl"])
                                else:
                                    S.op("dve", lambda E, ct=ct, jl=jl: E.tensor_tensor(actT[:, jl, :], asl[:], ct[:], op=ALU.mult),
                                         reads=[ck, "asl"], writes=["actT"])

                            dense_block(wup[0 if small_ffn else blk], KC, wring, lambda kc, c0, n: hn2[:, kc, c0:c0 + n], ["hn2"], segs, epi)
                    for eb in range(32 if p6_mode >= 3 else 0):
                        pm = P[pcount % 4]
                        pmk = "P%d" % (pcount % 4)
                        pcount += 1
                        segs = [(0, 512, pm, pmk, 0)]

                        def epi2(eb=eb, pm=pm, pmk=pmk, fh=fh, ydst=ydst):
                            yt, yk = yT.next()
                            S.op("act", lambda E, yt=yt, pm=pm: E.copy(yt[:], pm[:, 0:512]), reads=[pmk], writes=[yk])
                            import os as _os
                            _sub = int(_os.environ.get("P6SUB", "0"))
                            if _sub == 1:
                                return
                            yqt, yqk = yq.next()
                            if fh == 1 and _sub != 2:
                                ypt, ypk = yp.next()
                                S.dma("sp", ypt[:], ydst[:, :, eb * 128:(eb + 1) * 128], reads=["yS"], writes=[ypk], key="d_" + ypk)
                            for t in range(4):
                                pj = 5
                                pjk = "P5"
                                pc_ = 0
                                S.op("pe", lambda E, t=t, yt=yt, pc_=pc_: E.transpose(P[5][:, pc_:pc_ + 128], yt[:, t * 128:(t + 1) * 128], identf[:]),
                                     reads=[yk, "identf"], writes=[pjk])
                                if fh == 0 or _sub == 2:
                                    S.op("dve", lambda E, t=t, yqt=yqt, pc_=pc_: E.tensor_copy(yqt[:, t, :], P[5][:, pc_:pc_ + 128]),
                                         reads=[pjk], writes=[yqk])
                                else:
                                    S.op("dve", lambda E, t=t, yqt=yqt, ypt=ypt, pc_=pc_: E.tensor_tensor(
                                        yqt[:, t, :], P[5][:, pc_:pc_ + 128], ypt[:, t, :], op=ALU.add),
                                        reads=[pjk, ypk], writes=[yqk])
                            if _sub == 2:
                                return
                            S.dma("sp", ydst[:, :, eb * 128:(eb + 1) * 128], yqt[:], reads=[yqk], writes=["yS"], key="d_" + yqk)

                        dense_block(wdn[0 if small_ffn else eb][:, fh * FH:(fh + 1) * FH, :], FH, wring,
                                    lambda kc, c0, n: actT[:, kc, c0:c0 + n], ["actT"], segs, epi2)
                S.dma("sp", gt[:], g3.partition_broadcast(128), reads=[], writes=["gt"], key="c0")
                for t in range(4 if p6_mode >= 4 else 0):
                    row = half * 512 + t * 128
                    ht, hk = xring.next()
                    S.dma("sp", ht[:], hS[128 + row:128 + row + 128, :], reads=["hS"], writes=[hk], key="d_" + hk)
                    yt2, yk2 = xring.next()
                    S.dma("sp", yt2[:], yS[row:row + 128, :], reads=["yS"], writes=[yk2], key="d_" + yk2)
                    S.op("dve", lambda E, ht=ht, yt2=yt2: E.tensor_tensor(ht[:], ht[:], yt2[:], op=ALU.add), reads=[hk, yk2], writes=[hk])
                    ss, sk = ssring.next()
                    S.op("act", lambda E, yt2=yt2, ht=ht, ss=ss: E.activation(yt2[:], ht[:], AF.Square, accum_out=ss[:, 0:1]),
                         reads=[hk], writes=[yk2, sk])
                    S.op("dve", lambda E, ss=ss: E.tensor_scalar(ss[:, 1:2], ss[:, 0:1], 1.0 / D, 1e-6, op0=ALU.mult, op1=ALU.add),
                         reads=[sk], writes=[sk])
                    S.op("act", lambda E, ss=ss: E.activation(ss[:, 2:3], ss[:, 1:2], AF.Sqrt), reads=[sk], writes=[sk])
                    S.op("dve", lambda E, ss=ss: E.reciprocal(ss[:, 3:4], ss[:, 2:3]), reads=[sk], writes=[sk])
                    S.op("dve", lambda E, ss=ss, ht=ht, yt2=yt2: E.scalar_tensor_tensor(yt2[:], ht[:], ss[:, 3:4], gt[:], op0=ALU.mult, op1=ALU.mult),
                         reads=[hk, sk, "gt"], writes=[yk2])
                    S.dma("sp", out[row:row + 128, :], yt2[:], reads=[yk2], writes=["out"], key="dout")
            S.barrier()

    S.barrier()
    S.replay()
    st.close()
    return nc


DEBUG_OUT = set()


def _rel_bucket(n):
    n = np.maximum(n, 0)
    nf = np.maximum(n, 1).astype(np.float32)
    large = 16 + (np.log(nf / np.float32(16)) / np.float32(math.log(128 / 16)) * np.float32(16)).astype(np.int32)
    large = np.minimum(large, 31)
    return np.where(n < 16, n, large)


def _blk_layout(w, kcn):
    K, N = w.shape
    nb = N // 128
    return np.ascontiguousarray(w.reshape(kcn, 128, nb, 128).transpose(2, 1, 0, 3))


def _host_prep(inp):
    bf = ml_dtypes.bfloat16
    f = np.float32
    x = np.asarray(inp["x"], f)[0]
    shared = {}
    w_in = np.asarray(inp["w_in"], f)[0]
    w_in_p = np.zeros((D, NBLK_IN * 128), f)
    w_in_p[:, :w_in.shape[1]] = w_in
    shared["win"] = _blk_layout(w_in_p, KC)
    shared["wout"] = _blk_layout(np.asarray(inp["w_out"], f)[0], KC)
    shared["wup"] = _blk_layout(np.asarray(inp["w_up"], f)[0], KC)
    shared["wdn"] = _blk_layout(np.asarray(inp["w_down"], f)[0], FFC)
    shared["g1"] = np.asarray(inp["norm_mix_g"], f).reshape(1, D)
    shared["g2"] = np.asarray(inp["norm_ffn_g"], f).reshape(1, D)
    shared["g3"] = np.asarray(inp["norm_final_g"], f).reshape(1, D)
    pw = np.asarray(inp["pool_w"], f)[0]
    shared["poolw"] = np.ascontiguousarray(pw.reshape(4, 2, 128, 256).transpose(2, 0, 1, 3))
    shared["poolsc"] = np.ascontiguousarray(np.asarray(inp["pool_scale"], f)[0].reshape(8, 128).T)
    cw = np.asarray(inp["conv_w"], f)[0]
    shared["convw"] = np.ascontiguousarray(cw.T.reshape(172, 128, 3).transpose(1, 0, 2))
    shared["convb"] = np.ascontiguousarray(np.asarray(inp["conv_b"], f)[0].reshape(172, 128).T)
    for nm, a, b, c in (("k", "cmp_w1_k", "cmp_w2_k", "cmp_pe_k"), ("v", "cmp_w1_v", "cmp_w2_v", "cmp_pe_v")):
        shared["w1" + nm] = np.ascontiguousarray(np.asarray(inp[a], f)[0].transpose(1, 0, 2))
        shared["w2" + nm] = np.ascontiguousarray(np.asarray(inp[b], f)[0])
        shared["pe" + nm + "T"] = np.ascontiguousarray(np.asarray(inp[c], f)[0].T)
    tbl = np.asarray(inp["rel_bias"], f)
    k = np.arange(128)[:, None]
    q = np.arange(128)[None, :]
    idxA = _rel_bucket(q - k)
    idxB = _rel_bucket(q - k + 128)
    ba = np.stack([tbl[idxA], tbl[idxB]], axis=0)
    shared["ba"] = np.ascontiguousarray(ba.transpose(1, 3, 0, 2))
    shared["b31t"] = np.ascontiguousarray(np.broadcast_to(tbl[31][None, :], (128, HEADS)))
    ram = np.ones((128, 2, 128), f)
    ram[:, 0, :] = (q >= k)
    shared["ramask"] = ram
    jj = np.arange(15)[None, :]
    qq = np.arange(128)[:, None]
    dist = qq + 97 - 16 * jj
    shared["bband"] = np.ascontiguousarray(tbl[_rel_bucket(dist)].transpose(0, 2, 1))
    shared["bbmask"] = (dist >= 0).astype(f)
    shared["ident"] = np.eye(128).astype(bf)
    shared["identf"] = np.eye(128, dtype=f)
    shared["onesb"] = np.ones((128, 128)).astype(bf)
    wsl = np.zeros((512, 128), f)
    for j in range(128):
        for o, wgt in ((-1, 1.0), (0, 2.0), (1, 2.0), (2, 2.0), (3, 1.0)):
            n = 4 * j + o
            if 0 <= n < 512:
                wsl[n, j] = wgt
    shared["wslc"] = np.ascontiguousarray(wsl.reshape(4, 128, 128).transpose(1, 0, 2)).astype(bf)
    wm = np.zeros((128, 7, 384), f)
    kk = np.arange(128)[:, None]
    for r in range(7):
        for ii in range(3):
            dlt = 4 + ii - r
            if dlt == 0:
                blkm = (q >= kk)
            elif dlt == 4:
                blkm = (q < kk)
            elif 0 < dlt < 4:
                blkm = np.ones((128, 128), bool)
            else:
                blkm = np.zeros((128, 128), bool)
            wm[:, r, ii * 128:(ii + 1) * 128] = blkm
    shared["wmsk"] = wm.astype(bf)

    in_maps = []
    for c in range(NCORES):
        d = dict(shared)
        toff = 56 - 8 * c
        xfr = np.zeros((S_ALL, D), f)
        if toff > 0:
            xfr[toff * 128:] = x[:S_ALL - toff * 128]
        else:
            xfr[:] = x
        d["xf"] = xfr
        d["kbias"] = np.ascontiguousarray(np.broadcast_to(
            np.where(np.arange(NT) - toff >= 0, 0.0, NEGB).astype(f)[None, :], (128, NT)))
        nprime = np.arange(512)
        d["cex"] = (nprime - 8 * toff >= 0).astype(f).reshape(1, 512)
        mid = np.zeros((128, NQT, 128), f)
        cst = np.zeros((128, NQT, 128), f)
        vm = np.zeros((128, NQT, 128), f)
        jp = np.arange(128)[None, :]
        jg = jp - 2 * toff
        for i in range(NQT):
            t = 1024 * c - 128 + 128 * i + np.arange(128)[:, None]
            cur = t // 64
            invalid = (jg < 0) | (jg > cur) | (t < 0)
            f0 = (jg == 0) & ~invalid
            fc = (jg == cur) & ~invalid
            fp = (jg == cur - 1) & ~invalid
            forced = f0 | fc | fp
            midm = ~invalid & ~forced
            cs = np.where(invalid, -1e9 - 1e3 * jp, 0.0)
            cs = np.where(fp, 2e8, cs)
            cs = np.where(fc, 1e8, cs)
            cs = np.where(f0, 3e8, cs)
            mid[:, i, :] = midm
            cst[:, i, :] = cs
            vm[:, i, :] = ~invalid
        d["midmask"], d["cst"], d["vmask"] = mid, cst, vm
        ic = np.zeros((4, NQ), f)
        tq = 1024 * c - 128 + np.arange(NQ)
        for gi, w in enumerate((2, 4, 8, 16)):
            ic[gi] = 1.0 / np.minimum(np.maximum(tq, 0) + 1, w)
        d["invcnt"] = ic.reshape(1, -1)
        in_maps.append(d)
    return in_maps


_NC_CACHE = {}


def kernel(**inputs):
    in_maps = _host_prep(inputs)
    if "nc" not in _NC_CACHE:
        _NC_CACHE["nc"] = build_nc()
    nc = _NC_CACHE["nc"]
    res = run_bass_kernel_spmd(nc, in_maps, core_ids=list(range(NCORES)))
    outs = [np.asarray(r["out"], np.float32) for r in res.results]
    return np.concatenate(outs, axis=0).reshape(1, S_ALL, D)
```
